# Optimizing a Trainium2 kernel written in Bass

```python
import math
import jax, jax.numpy as jnp
from jax import lax
import numpy as np

D_MODEL = 1024
BATCH = 16
SEQ = 2048
DEPTH = 1

N_ATTN_HEADS = 8
HEAD_DIM = 64
D_ATTN = N_ATTN_HEADS * HEAD_DIM
D_CONV = D_MODEL - D_ATTN
D_MIX = D_ATTN + D_CONV
D_IN_PROJ = 3 * D_ATTN + 2 * D_CONV
CONV_WIDTH = 31
D_FF = 4 * D_MODEL
DILATED_BRANCHES = ((128, 1), (512, 4), (2048, 16))
NUM_BUCKETS = 32
MAX_DISTANCE = 2048
RMS_EPS = 1e-6
LN_EPS = 1e-5
NEG_INF = -1e30

kernel_name = "hybrid_dilated_attn_conformer_conv_layer"


def rmsnorm(x, g):
    xf = x.astype(jnp.float32)
    y = xf * lax.rsqrt(jnp.mean(xf * xf, axis=-1, keepdims=True) + RMS_EPS)
    return (y * g.astype(jnp.float32)).astype(x.dtype)


def layernorm(x, g, b):
    xf = x.astype(jnp.float32)
    mu = jnp.mean(xf, axis=-1, keepdims=True)
    var = jnp.mean(jnp.square(xf - mu), axis=-1, keepdims=True)
    y = (xf - mu) * lax.rsqrt(var + LN_EPS)
    return (y * g.astype(jnp.float32) + b.astype(jnp.float32)).astype(x.dtype)


def t5_causal_bucket(distance):
    max_exact = NUM_BUCKETS // 2
    d = jnp.maximum(distance, 1).astype(jnp.float32)
    large = max_exact + (jnp.log(d / max_exact) / math.log(MAX_DISTANCE / max_exact)
                         * (NUM_BUCKETS - max_exact)).astype(jnp.int32)
    large = jnp.minimum(large, NUM_BUCKETS - 1)
    return jnp.where(distance < max_exact, distance, large)


def dilated_window_branch(q, k, v, rel_bias, window, dilation):
    B, S, H, hd = q.shape
    W = window // dilation
    L = S // dilation
    nb = -(-L // W)
    Lp = nb * W
    Bd = B * dilation

    def to_sub(t):
        return t.reshape(B, L, dilation, H, hd).transpose(0, 2, 3, 1, 4).reshape(Bd, H, L, hd)

    qs, ks, vs = to_sub(q), to_sub(k), to_sub(v)
    qb = jnp.pad(qs, ((0, 0), (0, 0), (0, Lp - L), (0, 0))).reshape(Bd, H, nb, W, hd)

    def key_blocks(t):
        t = jnp.pad(t, ((0, 0), (0, 0), (W, Lp - L), (0, 0))).reshape(Bd, H, nb + 1, W, hd)
        return jnp.concatenate([t[:, :, :-1], t[:, :, 1:]], axis=3)

    kb, vb = key_blocks(ks), key_blocks(vs)

    qi = jnp.arange(W, dtype=jnp.int32)[:, None]
    kc = jnp.arange(2 * W, dtype=jnp.int32)[None, :]
    dist = qi + W - kc
    blk = jnp.arange(nb, dtype=jnp.int32)[:, None, None]
    key_idx = blk * W + kc[None] - W
    valid = ((dist >= 0) & (dist <= W))[None] & (key_idx >= 0)
    bucket = t5_causal_bucket(jnp.clip(dist, 0, W) * dilation)
    bias = rel_bias[bucket].astype(jnp.float32).transpose(2, 0, 1)

    scale = 1.0 / math.sqrt(hd)
    s = jnp.einsum('zhnqd,zhnkd->zhnqk', qb, kb).astype(jnp.float32) * scale + bias[:, None]
    s = jnp.where(valid, s, NEG_INF)
    m = jnp.max(s, axis=-1)
    p = jnp.exp(s - m[..., None])
    l = jnp.sum(p, axis=-1)
    o = jnp.einsum('zhnqk,zhnkd->zhnqd', p, vb.astype(jnp.float32)) / l[..., None]

    o = o.reshape(Bd, H, Lp, hd)[:, :, :L].reshape(B, dilation, H, L, hd)
    o = o.transpose(0, 3, 1, 2, 4).reshape(B, S, H, hd)
    m = m.reshape(Bd, H, Lp)[:, :, :L].reshape(B, dilation, H, L).transpose(0, 3, 1, 2).reshape(B, S, H)
    l = l.reshape(Bd, H, Lp)[:, :, :L].reshape(B, dilation, H, L).transpose(0, 3, 1, 2).reshape(B, S, H)
    return o, m, l


def dilated_attention(q, k, v, rel_bias):
    outs = [dilated_window_branch(q, k, v, rel_bias, w, d) for (w, d) in DILATED_BRANCHES]
    m_all = outs[0][1]
    for _, m_i, _ in outs[1:]:
        m_all = jnp.maximum(m_all, m_i)
    num = 0.0
    den = 0.0
    for o_i, m_i, l_i in outs:
        w_i = l_i * jnp.exp(m_i - m_all)
        num = num + w_i[..., None] * o_i
        den = den + w_i
    return num / den[..., None]


def conformer_conv(a, gate, conv_w, conv_b, ln_g, ln_b):
    u = a * jax.nn.sigmoid(gate)
    y = lax.conv_general_dilated(
        u, conv_w[:, None, :], window_strides=(1,), padding=((CONV_WIDTH - 1, 0),),
        dimension_numbers=('NWC', 'WIO', 'NWC'), feature_group_count=D_CONV)
    y = y + conv_b
    y = layernorm(y, ln_g, ln_b)
    return jax.nn.silu(y)


def setup_inputs(seed: int = 0) -> dict:
    key = jax.random.key(seed)
    ks = jax.random.split(key, 14)
    f32 = jnp.float32
    x = jax.random.normal(ks[0], (BATCH, SEQ, D_MODEL), f32)
    norm1_g = 1.0 + 0.02 * jax.random.normal(ks[1], (DEPTH, D_MODEL), f32)
    w_in = jax.random.normal(ks[2], (DEPTH, D_MODEL, D_IN_PROJ), f32) * D_MODEL ** -0.5
    conv_w = jax.random.normal(ks[3], (DEPTH, CONV_WIDTH, D_CONV), f32) * CONV_WIDTH ** -0.5
    conv_b = 0.02 * jax.random.normal(ks[4], (DEPTH, D_CONV), f32)
    conv_ln_g = 1.0 + 0.02 * jax.random.normal(ks[5], (DEPTH, D_CONV), f32)
    conv_ln_b = 0.02 * jax.random.normal(ks[6], (DEPTH, D_CONV), f32)
    w_o = jax.random.normal(ks[7], (DEPTH, D_MIX, D_MODEL), f32) * D_MIX ** -0.5
    norm2_g = 1.0 + 0.02 * jax.random.normal(ks[8], (DEPTH, D_MODEL), f32)
    w_ff1 = jax.random.normal(ks[9], (DEPTH, D_MODEL, D_FF), f32) * D_MODEL ** -0.5
    w_ff2 = jax.random.normal(ks[10], (DEPTH, D_FF, D_MODEL), f32) * D_FF ** -0.5
    rel_bias = 0.5 * jax.random.normal(ks[11], (NUM_BUCKETS, N_ATTN_HEADS), f32)
    final_g = 1.0 + 0.02 * jax.random.normal(ks[12], (D_MODEL,), f32)
    return {"x": x, "norm1_g": norm1_g, "w_in": w_in, "conv_w": conv_w, "conv_b": conv_b,
            "conv_ln_g": conv_ln_g, "conv_ln_b": conv_ln_b, "w_o": w_o, "norm2_g": norm2_g,
            "w_ff1": w_ff1, "w_ff2": w_ff2, "rel_bias": rel_bias, "final_g": final_g}


def reference(x, norm1_g, w_in, conv_w, conv_b, conv_ln_g, conv_ln_b, w_o, norm2_g,
              w_ff1, w_ff2, rel_bias, final_g):
    B, S, _ = x.shape
    split_pts = [D_ATTN, 2 * D_ATTN, 3 * D_ATTN, 3 * D_ATTN + D_CONV]
    for layer in range(DEPTH):
        h = rmsnorm(x, norm1_g[layer])
        z = jnp.einsum('bsd,de->bse', h, w_in[layer])
        q, k, v, a, gate = jnp.split(z, split_pts, axis=-1)
        q = q.reshape(B, S, N_ATTN_HEADS, HEAD_DIM)
        k = k.reshape(B, S, N_ATTN_HEADS, HEAD_DIM)
        v = v.reshape(B, S, N_ATTN_HEADS, HEAD_DIM)
        attn = dilated_attention(q, k, v, rel_bias).astype(x.dtype).reshape(B, S, D_ATTN)
        conv = conformer_conv(a, gate, conv_w[layer], conv_b[layer],
                              conv_ln_g[layer], conv_ln_b[layer])
        mixed = jnp.concatenate([attn, conv], axis=-1)
        x = x + jnp.einsum('bse,ed->bsd', mixed, w_o[layer])
        h = rmsnorm(x, norm2_g[layer])
        f = jnp.square(jax.nn.relu(jnp.einsum('bsd,df->bsf', h, w_ff1[layer])))
        x = x + jnp.einsum('bsf,fd->bsd', f, w_ff2[layer])
    return rmsnorm(x, final_g)
```

```python
import math
from contextlib import ExitStack

import numpy as np
import concourse.bass as bass
import concourse.mybir as mybir
from concourse.bass_utils import run_bass_kernel_spmd

F32 = mybir.dt.float32
BF16 = mybir.dt.bfloat16
AF = mybir.ActivationFunctionType
ALU = mybir.AluOpType

P = 128
S = 2048
D = 1024
NT = S // P
DFF = 4096
NH = 8
HD = 64
DIN = 2560
NCORES = 8
EPS_RMS = 1e-6
EPS_LN = 1e-5
CW = 31
DILS = (1, 4, 16)
NFG = 8

_DSZ = {F32: 4, BF16: 2}


def _dsz(dt):
    return _DSZ[dt]


class Ins:
    __slots__ = ("eng", "fn", "deps", "order", "needs_inc", "inc_val", "dma_key", "dma_val")


SMALL_BASE = 1 << 30


class Tracker:
    ENGS = ("pe", "act", "dve", "pool", "sp")

    def __init__(self):
        self.streams = {e: [] for e in self.ENGS}
        self.lastw = {}
        self.readers = {}
        self.dma_counts = {}
        self.order = 0
        self.small_base = SMALL_BASE

    def cells(self, ap):
        t = ap.tensor
        sp = str(t.space)
        if "SB" in sp.upper() or "SBUF" in sp.upper():
            base = t.manual_sbuf_range[0]
            tag = "s"
        elif "PSUM" in sp.upper() or "PS" in sp.upper():
            base = 0
            tag = t.name
        else:
            return ()
        esz = _dsz(ap.dtype)
        pbytes = 1
        for d in t.shape[1:]:
            pbytes *= int(d)
        pbytes *= _dsz(t.dtype)
        off = (int(ap.offset) * esz) % pbytes
        lo = off
        hi = off
        for step, cnt in ap.ap[1:]:
            step = int(step) * esz
            cnt = int(cnt)
            if step >= 0:
                hi += step * (cnt - 1)
            else:
                lo += step * (cnt - 1)
        hi += esz
        lo += base
        hi += base
        if tag == "s" and lo >= self.small_base:
            g = 32
        else:
            g = 512
        return [(tag, g, c) for c in range(lo // g, (hi - 1) // g + 1)]

    def add(self, eng, fn, reads=(), writes=(), dma_key=None, ptr_reads=()):
        ins = Ins()
        ins.eng = eng
        ins.fn = fn
        ins.order = self.order
        self.order += 1
        ins.needs_inc = False
        ins.inc_val = 0
        ins.dma_key = dma_key
        ins.dma_val = 0
        if dma_key is not None:
            self.dma_counts[dma_key] = self.dma_counts.get(dma_key, 0) + 16
            ins.dma_val = self.dma_counts[dma_key]
        self.streams[eng].append(ins)
        deps = {}

        def need(p, ptr=False):
            if p is ins:
                return
            if p.dma_key is None and dma_key is None and p.eng == eng and not ptr:
                return
            k = ("d", p.dma_key) if p.dma_key is not None else ("e", p.eng)
            q = deps.get(k)
            if q is None or q.order < p.order:
                deps[k] = p

        rc = []
        for ap in reads:
            rc.extend(self.cells(ap))
        wc = []
        for ap in writes:
            wc.extend(self.cells(ap))
        for c in rc:
            w = self.lastw.get(c)
            if w is not None:
                need(w)
        for ap in ptr_reads:
            cs = self.cells(ap)
            rc.extend(cs)
            for c in cs:
                w = self.lastw.get(c)
                if w is not None:
                    need(w, True)
        for c in wc:
            w = self.lastw.get(c)
            if w is not None:
                need(w)
            rd = self.readers.get(c)
            if rd:
                for r in rd.values():
                    need(r)
        rk = eng if dma_key is None else ("dma", dma_key)
        for c in rc:
            self.readers.setdefault(c, {})[rk] = ins
        for c in wc:
            self.lastw[c] = ins
            self.readers[c] = {}
        for p in deps.values():
            if p.dma_key is None:
                p.needs_inc = True
        ins.deps = deps
        return ins

    def finalize(self):
        for e, st in self.streams.items():
            cnt = 0
            for ins in st:
                if ins.needs_inc:
                    cnt += 1
                    ins.inc_val = cnt


class Builder:
    def __init__(self, nseq, dump=()):
        self.nseq = nseq
        self.dump_names = set(dump)
        self.dumps = {}
        self.nc = bass.Bass("TRN2", target_bir_lowering=False)
        self.tk = Tracker()
        self.psum_rr = 0
        self._alloc()

    def _sb(self, name, shape, dt, off):
        return self.nc.alloc_sbuf_tensor_at(name, list(shape), dt, offset=off)

    def _alloc(self):
        nc = self.nc
        ns = self.nseq
        dr = lambda n, sh: nc.dram_tensor(n, list(sh), F32, kind="ExternalInput").ap()
        self.x_d = dr("x", [ns, S, D])
        self.w_in_d = dr("w_in", [D, DIN])
        self.w_o_d = dr("w_o", [D, D])
        self.w1_d = dr("w_ff1", [D, DFF])
        self.w2_d = dr("w_ff2", [DFF, D])
        self.cols_d = dr("cols", [P, 160])
        self.gfin_d = dr("gfin", [P, D])
        self.bias_d = dr("biasT", [P, 3, NH, 256])
        self.mask_d = dr("mask", [P, 3, 256])
        self.ident_d = dr("ident", [P, P])
        self.out_d = nc.dram_tensor("out", [ns, S, D], F32, kind="ExternalOutput").ap()

        off = 16896
        self.hT = self._sb("hT", [P, 8, S], BF16, off); off += 32768
        R1 = off; off += 65536
        R2 = off; off += 32768
        R3 = off; off += 32768
        STG = off; off += 16384
        HX = off; off += 8192
        self.xres = self._sb("xres", [P, NT, D], F32, R1)
        o = R1
        self.uT = []
        for i in range(2):
            self.uT.append(self._sb(f"uT{i}", [P, 2080], BF16, o)); o += 4160
        self.yT = self._sb("yT", [P, 4, S], F32, o); o += 32768
        self.diag = self._sb("diag", [P, CW, P], BF16, o); o += 7936
        self.sg = []
        for i in range(2):
            self.sg.append(self._sb(f"sg{i}", [P, 512], F32, o)); o += 2048
        self.ysq = []
        for i in range(2):
            self.ysq.append(self._sb(f"ysq{i}", [P, 512], F32, o)); o += 2048
        self.ln_mean = self._sb("ln_mean", [P, 512], F32, o); o += 2048
        self.ln_rstd = self._sb("ln_rstd", [P, 512], F32, o); o += 2048
        self.ln_z = []
        for i in range(2):
            self.ln_z.append(self._sb(f"ln_z{i}", [P, 512], F32, o)); o += 2048
        assert o <= R1 + 65536, o - R1
        o = R1
        self.qT = self._sb("qT", [P, S], BF16, o); o += 4096
        self.kT = self._sb("kT", [P, S], BF16, o); o += 4096
        self.vT = self._sb("vT", [P, S], BF16, o); o += 4096
        self.vblk = self._sb("vblk", [P, 3, 16, P], BF16, o); o += 12288
        self.acc = self._sb("acc", [P, 2, S], F32, o); o += 16384
        self.tab = self._sb("tab", [P, 3, NH, 256], BF16, o); o += 12288
        self.praw = [[None, None], [None, None]]
        self.pT = [[None, None], [None, None]]
        for h in range(2):
            for b in range(2):
                self.praw[h][b] = self._sb(f"praw{h}{b}", [P, 512], BF16, o); o += 1024
        for h in range(2):
            for b in range(2):
                self.pT[h][b] = self._sb(f"pT{h}{b}", [P, 512], BF16, o); o += 1024
        assert o <= R1 + 65536
        self.mixedT = self._sb("mixedT", [P, 8, S], BF16, R2)
        self.fT = self._sb("fT", [P, 4, S], BF16, R2)
        self.rtmp = [self._sb(f"rtmp{i}", [P, 512], BF16, R2 + 16384 + 1024 * i) for i in range(2)]
        self.wslot = [self._sb(f"wslot{i}", [P, 3, 8, P], BF16, R3 + 6144 * i) for i in range(2)]
        self.wo = self._sb("wo", [P, 8, D], BF16, R3 + 16384)
        self.w1g = [self._sb(f"w1g{i}", [P, 8, 512], BF16, R3 + 16384 * i) for i in range(2)]
        self.w2g = [self._sb(f"w2g{i}", [P, 4, D], BF16, R3 + 16384 * i + 8192) for i in range(2)]
        self.xst = [self._sb(f"xst{i}", [P, D], F32, STG + 4096 * i) for i in range(2)]
        self.ost = [self._sb(f"ost{i}", [P, D], F32, STG + 8192 + 4096 * i) for i in range(2)]
        self.bstage = self._sb("bstage", [P, NH, 256], F32, STG)
        self.hx = [self._sb(f"hx{i}", [P, D], BF16, HX + 2048 * i) for i in range(4)]
        self.gfin = self._sb("gfin", [P, D], F32, off); off += 4096
        self.mask = self._sb("maskc", [P, 3, 256], F32, off); off += 3072
        self.junk = self._sb("junk", [P, D], BF16, off); off += 2048
        self.ones_f = self._sb("ones_f", [P, P], F32, off); off += 512
        self.ident = self._sb("identb", [P, P], BF16, off); off += 512
        self.ones_b = self._sb("ones_b", [P, 64], BF16, off); off += 512
        self.cols = self._sb("colsc", [P, 160], F32, off); off += 1024
        self.tk.small_base = off
        self.ss = []
        self.rs = []
        for i in range(8):
            self.ss.append(self._sb(f"ss{i}", [P, 1], F32, off)); off += 32
        for i in range(8):
            self.rs.append(self._sb(f"rs{i}", [P, 1], F32, off)); off += 32
        self.ss_i = 0
        assert off <= 229312, off
        self.ps = [nc.alloc_psum_tensor(f"ps{i}", [P, 512], F32) for i in range(8)]

    def g1col(self, c): return self.cols[:, c:c + 1]
    def g2col(self, c): return self.cols[:, 8 + c:9 + c]
    def convw(self, cc): return self.cols[:, 16 + CW * cc:16 + CW * (cc + 1)]
    def convb(self, cc): return self.cols[:, 140 + cc:141 + cc]
    def lng(self, cc): return self.cols[:, 144 + cc:145 + cc]
    def lnb(self, cc): return self.cols[:, 148 + cc:149 + cc]
    def eps_rms(self): return self.cols[:, 152:153]
    def eps_ln(self): return self.cols[:, 153:154]

    def psum_next(self):
        b = self.ps[self.psum_rr % 8]
        self.psum_rr += 1
        return b

    def mm(self, out, lhsT, rhs, start, stop):
        rd = [lhsT, rhs] + ([] if start else [out])
        self.tk.add("pe", lambda e: e.matmul(out, lhsT, rhs, start=start, stop=stop), rd, [out])

    def tr(self, out, in_):
        ident = self.ident[:, :]
        self.tk.add("pe", lambda e: e.transpose(out, in_, ident), [in_, ident], [out])

    def act(self, out, in_, func, bias=None, scale=None, accum_out=None):
        kw = {}
        rd = [in_]
        wr = [out]
        pr = []
        if bias is not None:
            kw["bias"] = bias
            if not isinstance(bias, (int, float)):
                pr.append(bias)
        if scale is not None:
            kw["scale"] = scale
            if not isinstance(scale, (int, float)):
                pr.append(scale)
        if accum_out is not None:
            kw["accum_out"] = accum_out
            wr.append(accum_out)
        self.tk.add("act", lambda e: e.activation(out, in_, func, **kw), rd, wr, ptr_reads=pr)

    def tt(self, eng, out, in0, in1, op):
        self.tk.add(eng, lambda e: e.tensor_tensor(out, in0, in1, op), [in0, in1], [out])

    def ts(self, eng, out, in0, s1, s2, op0, op1=None):
        rd = [in0]
        pr = []
        for s in (s1, s2):
            if s is not None and not isinstance(s, (int, float)):
                pr.append(s)
        if op1 is None:
            fn = lambda e: e.tensor_scalar(out, in0, s1, s2, op0)
        else:
            fn = lambda e: e.tensor_scalar(out, in0, s1, s2, op0, op1)
        self.tk.add(eng, fn, rd, [out], ptr_reads=pr)

    def stt(self, eng, out, in0, scalar, in1, op0, op1):
        rd = [in0, in1]
        pr = []
        if not isinstance(scalar, (int, float)):
            pr.append(scalar)
        self.tk.add(eng, lambda e: e.scalar_tensor_tensor(out, in0, scalar, in1, op0, op1), rd, [out], ptr_reads=pr)

    def cp(self, eng, out, in_):
        if eng == "act":
            self.act(out, in_, AF.Copy)
        else:
            self.tk.add(eng, lambda e: e.tensor_copy(out, in_), [in_], [out])

    def recip(self, eng, out, in_):
        self.tk.add(eng, lambda e: e.reciprocal(out, in_), [in_], [out])

    def memset(self, eng, ap, val):
        self.tk.add(eng, lambda e: e.memset(ap, val), [], [ap])

    def dma(self, q, out, in_, key):
        self.tk.add(q, lambda e: e.dma_start(out=out, in_=in_), [in_], [out], dma_key=key)

    def dump(self, name, ap, dt=None):
        if name not in self.dump_names:
            return
        dt = dt or ap.dtype
        shape = [int(s) for s in ap.shape]
        d = self.nc.dram_tensor("dbg_" + name, shape, dt, kind="ExternalOutput").ap()
        self.dumps[name] = d
        self.dma("sp", d, ap, ("dbg", name))

    def load_consts(self):
        self.dma("sp", self.cols[:, :], self.cols_d, ("c", "cols"))
        self.dma("sp", self.gfin[:, :], self.gfin_d, ("c", "gfin"))
        self.dma("sp", self.mask[:, :, :], self.mask_d, ("c", "mask"))
        self.dma("pool", self.ident[:, :], self.ident_d, ("c", "ident"))
        self.memset("dve", self.ones_f[:, :], 1.0)
        self.memset("dve", self.ones_b[:, :], 1.0)

    def norm_tile(self, src, hx):
        i = self.ss_i % 8
        self.ss_i += 1
        ss = self.ss[i][:, :]
        rs = self.rs[i][:, :]
        self.act(self.junk[:, :], src, AF.Square, accum_out=ss)
        self.act(rs, ss, AF.Sqrt, bias=self.eps_rms(), scale=1.0 / D)
        self.recip("dve", rs, rs)
        self.act(hx, src, AF.Copy, scale=rs)
        return rs

    def transposes4(self, t0, gcolfn):
        for c in range(8):
            ps = self.psum_next()
            psb = ps[:, :].bitcast(BF16)
            for i in range(4):
                self.tr(psb[:, P * i:P * (i + 1)], self.hx[i][:, P * c:P * (c + 1)])
            self.ts("dve", self.hT[:, c, P * t0:P * t0 + 512], psb[:, 0:512], gcolfn(c), None, ALU.mult)

    def phase_norm1(self, s):
        for t in range(NT):
            xs = self.xst[t % 2]
            self.dma("sp", xs[:, :], self.x_d[s, P * t:P * (t + 1), :], ("ld", t % 2))
            self.norm_tile(xs[:, :], self.hx[t % 4][:, :])
            if t % 4 == 3:
                self.transposes4(t - 3, self.g1col)

    def load_wslot(self, slot, colbases):
        w = self.w_in_d.rearrange("(c p) n -> p c n", p=P)
        for i, cb in enumerate(colbases):
            self.dma("pool", self.wslot[slot][:, i, :, :], w[:, :, cb:cb + P], ("w", slot, i))

    def inproj_fm(self, slot, i, evac):
        for tg in range(4):
            ps = self.psum_next()
            for c in range(8):
                self.mm(ps[:, :], self.wslot[slot][:, i, c, :], self.hT[:, c, 512 * tg:512 * (tg + 1)], c == 0, c == 7)
            evac(tg, ps)

    def phase_conv(self, s, slot0):
        ev = 0
        for cc in range(4):
            slot = (slot0 + cc) % 2
            self.load_wslot(slot, [1536 + P * cc, 2048 + P * cc])
            uT = self.uT[cc % 2]
            self.memset("dve", uT[:, 0:32], 0.0)
            pa = [None] * 4

            def ev_a(tg, ps):
                pa[tg] = ps

            for tg in range(4):
                psa = self.psum_next()
                for c in range(8):
                    self.mm(psa[:, :], self.wslot[slot][:, 0, c, :], self.hT[:, c, 512 * tg:512 * (tg + 1)], c == 0, c == 7)
                psg = self.psum_next()
                for c in range(8):
                    self.mm(psg[:, :], self.wslot[slot][:, 1, c, :], self.hT[:, c, 512 * tg:512 * (tg + 1)], c == 0, c == 7)
                sg = self.sg[tg % 2]
                self.act(sg[:, :], psg[:, :], AF.Sigmoid)
                self.tt("dve", uT[:, 30 + 512 * tg:30 + 512 * (tg + 1)], psa[:, :], sg[:, :], ALU.mult)
            idb = self.ident[:, :].unsqueeze(1).broadcast_to([P, CW, P])
            wb = self.convw(cc).unsqueeze(2).broadcast_to([P, CW, P])
            self.tt("dve", self.diag[:, :, :], idb, wb, ALU.mult)
            for tg in range(4):
                psy = self.psum_next()
                for j in range(CW):
                    self.mm(psy[:, :], self.diag[:, j, :], uT[:, 512 * tg + j:512 * tg + j + 512], j == 0, j == CW - 1)
                self.act(self.yT[:, cc, 512 * tg:512 * (tg + 1)], psy[:, :], AF.Identity, bias=self.convb(cc))
        self.dump(f"yT{s}", self.yT[:, :, :])
        for tg in range(4):
            tsl = slice(512 * tg, 512 * (tg + 1))
            ps_sum = self.psum_next()
            ps_sq = self.psum_next()
            for cc in range(4):
                self.mm(ps_sum[:, :], self.ones_f[:, :], self.yT[:, cc, tsl], cc == 0, cc == 3)
            for cc in range(4):
                q = self.ysq[cc % 2]
                self.act(q[:, :], self.yT[:, cc, tsl], AF.Square)
                self.mm(ps_sq[:, :], self.ones_f[:, :], q[:, :], cc == 0, cc == 3)
            mean = self.ln_mean[:, :]
            rstd = self.ln_rstd[:, :]
            self.ts("dve", mean, ps_sum[:, :], 1.0 / 512, None, ALU.mult)
            self.tt("dve", rstd, mean, mean, ALU.mult)
            self.stt("dve", rstd, ps_sq[:, :], 1.0 / 512, rstd, ALU.mult, ALU.subtract)
            self.act(rstd, rstd, AF.Sqrt, bias=self.eps_ln())
            self.recip("dve", rstd, rstd)
            for cc in range(4):
                z = self.ln_z[cc % 2][:, :]
                self.tt("dve", z, self.yT[:, cc, tsl], mean, ALU.subtract)
                self.tt("dve", z, z, rstd, ALU.mult)
                self.act(self.mixedT[:, 4 + cc, tsl], z, AF.Silu, scale=self.lng(cc), bias=self.lnb(cc))

    def gen_tables(self):
        for di in range(3):
            self.dma("sp", self.bstage[:, :, :], self.bias_d[:, di, :, :], ("bst",))
            self.act(self.bstage[:, :, :], self.bstage[:, :, :], AF.Exp)
            mb = self.mask[:, di, :].unsqueeze(1).broadcast_to([P, NH, 256])
            self.tt("dve", self.tab[:, di, :, :], self.bstage[:, :, :], mb, ALU.mult)

    @staticmethod
    def tok_slice(d, r, n):
        start = d * P * n + r
        return slice(start, start + d * (P - 1) + 1, d)

    def phase_attn_pair(self, s, p, slot):
        self.load_wslot(slot, [P * p, 512 + P * p, 1024 + P * p])
        dsts = [self.qT, self.kT, self.vT]
        evi = [0]
        for i in range(3):
            dst = dsts[i]

            def evac(tg, ps, dst=dst):
                eng = "act" if evi[0] % 2 == 0 else "dve"
                evi[0] += 1
                self.cp(eng, dst[:, 512 * tg:512 * (tg + 1)], ps[:, :])

            self.inproj_fm(slot, i, evac)
        if p == 0:
            self.dump(f"qT{s}", self.qT[:, :])
            self.dump(f"kT{s}", self.kT[:, :])
            self.dump(f"vT{s}", self.vT[:, :])
        for di, d in enumerate(DILS):
            nb = 16 // d
            for g in range(4):
                ps = self.psum_next()
                psb = ps[:, :].bitcast(BF16)
                for j in range(4):
                    vb = 4 * g + j
                    r, n = vb // nb, vb % nb
                    self.tr(psb[:, P * j:P * (j + 1)], self.vT[:, self.tok_slice(d, r, n)])
                eng = "act" if g % 2 == 0 else "dve"
                self.cp(eng, self.vblk[:, di, 4 * g:4 * g + 4, :], psb[:, 0:512].rearrange("p (a b) -> p a b", a=4))
        sgroups = []
        for di, d in enumerate(DILS):
            nb = 16 // d
            units = [(r, n) for n in range(nb) for r in range(d)] if di == 1 else \
                    [(r, n) for r in range(d) for n in range(nb)]
            per = 2 if di < 2 else 4
            for i in range(0, len(units), per):
                sgroups.append((di, units[i:i + per], i // 4, (i % 4)))
        sbank = lambda gi, h: self.ps[2 * (gi % 2) + h]
        obank = lambda og, w: self.ps[4 + 2 * (og % 2) + w]

        ogc = [0]

        def emit_qk(gi):
            di, units, _, _ = sgroups[gi]
            d = DILS[di]
            for h in range(2):
                sb = sbank(gi, h)
                hs = slice(64 * h, 64 * h + 64)
                for j, (r, n) in enumerate(units):
                    qs = self.tok_slice(d, r, n)
                    if di < 2:
                        if n > 0:
                            self.mm(sb[:, 256 * j:256 * j + P], self.kT[hs, self.tok_slice(d, r, n - 1)], self.qT[hs, qs], True, True)
                        self.mm(sb[:, 256 * j + P:256 * j + 256], self.kT[hs, qs], self.qT[hs, qs], True, True)
                    else:
                        self.mm(sb[:, P * j:P * (j + 1)], self.kT[hs, qs], self.qT[hs, qs], True, True)

        def emit_sm(gi):
            di, units, _, _ = sgroups[gi]
            for h in range(2):
                sb = sbank(gi, h)
                pr = self.praw[h][gi % 2]
                pt = self.pT[h][gi % 2]
                hh = 2 * p + h
                if di < 2:
                    for j, (r, n) in enumerate(units):
                        lo = 256 * j + (0 if n > 0 else P)
                        hi = 256 * j + 256
                        self.act(pr[:, lo:hi], sb[:, lo:hi], AF.Exp, scale=0.125)
                        self.tt("dve", pt[:, lo:hi], pr[:, lo:hi], self.tab[:, di, hh, lo - 256 * j:256], ALU.mult)
                else:
                    w = P * len(units)
                    self.act(pr[:, 0:w], sb[:, 0:w], AF.Exp, scale=0.125)
                    tb = self.tab[:, di, hh, P:256].unsqueeze(1).broadcast_to([P, len(units), P])
                    self.tt("dve", pt[:, 0:w].rearrange("p (a b) -> p a b", b=P),
                            pr[:, 0:w].rearrange("p (a b) -> p a b", b=P), tb, ALU.mult)

        def emit_pv(gi):
            di, units, og_local, j0 = sgroups[gi]
            d = DILS[di]
            nb = 16 // d
            og = ogc[0]
            onum = obank(og, 0)
            oden = obank(og, 1)
            for h in range(2):
                pt = self.pT[h][gi % 2]
                hs = slice(64 * h, 64 * h + 64)
                for j, (r, n) in enumerate(units):
                    oc = slice(P * (j0 + j), P * (j0 + j + 1))
                    tiles = []
                    if di < 2:
                        if n > 0:
                            tiles.append((pt[:, 256 * j:256 * j + P], r * nb + n - 1))
                        tiles.append((pt[:, 256 * j + P:256 * j + 256], r * nb + n))
                    else:
                        tiles.append((pt[:, P * j:P * (j + 1)], r * nb + n))
                    for k, (pap, vb) in enumerate(tiles):
                        self.mm(onum[hs, oc], self.vblk[:, di, vb, hs], pap, k == 0, k == len(tiles) - 1)
                    for k, (pap, vb) in enumerate(tiles):
                        self.mm(oden[hs, oc], self.ones_b[:, :], pap, k == 0, k == len(tiles) - 1)
            if j0 + len(units) == 4:
                u0 = units[-1]
                if di == 0:
                    g = og_local
                    self.cp("act", self.acc[:, 0, 512 * g:512 * (g + 1)], onum[:, :])
                    self.cp("dve", self.acc[:, 1, 512 * g:512 * (g + 1)], oden[:, :])
                else:
                    if di == 1:
                        n = u0[1]
                        v = lambda w: self.acc[:, w, 512 * n:512 * (n + 1)].rearrange("p (i r) -> p r i", r=4)
                    else:
                        g = og_local
                        v = lambda w: self.acc[:, w, :].rearrange("p (i r) -> p r i", r=16)[:, 4 * g:4 * g + 4, :]
                    o3 = lambda b: b[:, :].rearrange("p (r i) -> p r i", r=4)
                    self.tt("dve", v(0), v(0), o3(onum), ALU.add)
                    self.tt("dve", v(1), v(1), o3(oden), ALU.add)
                ogc[0] += 1

        ng = len(sgroups)
        for gi in range(ng):
            emit_qk(gi)
            emit_sm(gi)
            if gi >= 1:
                emit_pv(gi - 1)
        emit_pv(ng - 1)
        self.recip("dve", self.acc[:, 1, :], self.acc[:, 1, :])
        self.tt("dve", self.mixedT[:, p, :], self.acc[:, 0, :], self.acc[:, 1, :], ALU.mult)

    def phase_wo(self, s):
        self.dma("pool", self.wo[:, :, :], self.w_o_d.rearrange("(c p) n -> p c n", p=P), ("wo",))

    def phase_wo_compute(self, s):
        for t in range(NT):
            xs = self.xst[t % 2]
            self.dma("sp", xs[:, :], self.x_d[s, P * t:P * (t + 1), :], ("ld", t % 2))
            for half in range(2):
                hsl = slice(512 * half, 512 * (half + 1))
                ps = self.psum_next()
                for c in range(8):
                    self.mm(ps[:, :], self.mixedT[:, c, P * t:P * (t + 1)], self.wo[:, c, hsl], c == 0, c == 7)
                self.tt("dve", self.xres[:, t, hsl], ps[:, :], xs[:, hsl], ALU.add)
            self.norm_tile(self.xres[:, t, :], self.hx[t % 4][:, :])
            if t % 4 == 3:
                self.transposes4(t - 3, self.g2col)

    def load_ffn_group(self, g):
        sl = g % 2
        w1 = self.w1_d.rearrange("(c p) n -> p c n", p=P)
        self.dma("pool", self.w1g[sl][:, :, :], w1[:, :, 512 * g:512 * (g + 1)], ("w1", sl))
        w2 = self.w2_d.rearrange("(c p) n -> p c n", p=P)
        self.dma("pool", self.w2g[sl][:, :, :], w2[:, 4 * g:4 * g + 4, :], ("w2", sl))

    def phase_ffn(self, s, prefetched0):
        if not prefetched0:
            self.load_ffn_group(0)
        ri = 0
        for g in range(NFG):
            if g + 1 < NFG:
                self.load_ffn_group(g + 1)
            sl = g % 2
            for tg in range(4):
                for fc in range(4):
                    ps = self.psum_next()
                    for c in range(8):
                        self.mm(ps[:, :], self.w1g[sl][:, c, P * fc:P * (fc + 1)], self.hT[:, c, 512 * tg:512 * (tg + 1)], c == 0, c == 7)
                    rt = self.rtmp[ri % 2]
                    ri += 1
                    self.act(rt[:, :], ps[:, :], AF.Relu)
                    self.tt("dve", self.fT[:, fc, 512 * tg:512 * (tg + 1)], rt[:, :], rt[:, :], ALU.mult)
            for t in range(NT):
                for half in range(2):
                    hsl = slice(512 * half, 512 * (half + 1))
                    ps = self.psum_next()
                    for fc in range(4):
                        self.mm(ps[:, :], self.fT[:, fc, P * t:P * (t + 1)], self.w2g[sl][:, fc, hsl], fc == 0, fc == 3)
                    self.tt("dve", self.xres[:, t, hsl], self.xres[:, t, hsl], ps[:, :], ALU.add)
                if g == NFG - 1:
                    self.final_tile(s, t)

    def final_tile(self, s, t):
        i = self.ss_i % 8
        self.ss_i += 1
        ss = self.ss[i][:, :]
        rs = self.rs[i][:, :]
        src = self.xres[:, t, :]
        self.act(self.junk[:, :], src, AF.Square, accum_out=ss)
        self.act(rs, ss, AF.Sqrt, bias=self.eps_rms(), scale=1.0 / D)
        self.recip("dve", rs, rs)
        o = self.ost[t % 2]
        self.stt("dve", o[:, :], src, rs, self.gfin[:, :], ALU.mult, ALU.mult)
        self.dma("sp", self.out_d[s, P * t:P * (t + 1), :], o[:, :], ("st", t % 2))

    def build(self):
        self.load_consts()
        slot = 0
        for s in range(self.nseq):
            self.phase_norm1(s)
            self.dump(f"hT{s}", self.hT[:, :, :])
            self.phase_wo(s)
            self.phase_conv(s, slot)
            self.gen_tables()
            for p in range(4):
                self.phase_attn_pair(s, p, (slot + p) % 2)
            self.dump(f"mixedT{s}", self.mixedT[:, :, :])
            self.phase_wo_compute(s)
            self.dump(f"x1_{s}", self.xres[:, :, :])
            self.dump(f"h2T{s}", self.hT[:, :, :])
            self.phase_ffn(s, False)
        self.emit()
        return self.nc

    def emit(self):
        nc = self.nc
        tk = self.tk
        tk.finalize()
        with ExitStack() as es:
            esem = {e: es.enter_context(nc.semaphore(f"sem_{e}")) for e in ("pe", "act", "dve", "pool")}
            dsem = {}
            for i, k in enumerate(tk.dma_counts):
                dsem[k] = es.enter_context(nc.semaphore(f"dsem{i}"))
            block = es.enter_context(nc.Block())

            def replay(ename, eng, final=False):
                waited = {}
                for ins in tk.streams[ename]:
                    for k, pr in ins.deps.items():
                        if k[0] == "d":
                            sem = dsem[k[1]]
                            val = pr.dma_val
                        else:
                            sem = esem[k[1]]
                            val = pr.inc_val
                        if waited.get(k, 0) < val:
                            eng.wait_ge(sem, val)
                            waited[k] = val
                    bi = ins.fn(eng)
                    if ins.dma_key is not None:
                        bi.then_inc(dsem[ins.dma_key], 16)
                    elif ins.needs_inc:
                        bi.then_inc(esem[ename], 1)
                if final:
                    for k, cnt in tk.dma_counts.items():
                        if k[0] in ("st", "dbg"):
                            eng.wait_ge(dsem[k], cnt)

            @block.tensor
            def _(pe):
                replay("pe", pe)

            @block.scalar
            def _(a):
                replay("act", a)

            @block.vector
            def _(v):
                replay("dve", v)

            @block.gpsimd
            def _(g):
                replay("pool", g)

            @block.sync
            def _(sp):
                replay("sp", sp, final=True)


def _t5_bucket(distance):
    max_exact = 16
    d = np.maximum(distance, 1).astype(np.float32)
    large = max_exact + (np.log(d / np.float32(max_exact)) / np.float32(math.log(2048 / max_exact))
                         * np.float32(32 - max_exact)).astype(np.int32)
    large = np.minimum(large, 31)
    return np.where(distance < max_exact, distance, large)


def _bias_tables(rel_bias):
    k = np.arange(P)[:, None]
    q = np.arange(P)[None, :]
    biasT = np.zeros((P, 3, NH, 256), np.float32)
    mask = np.zeros((P, 3, 256), np.float32)
    for di, d in enumerate(DILS):
        for blk in range(2):
            dist = q + P - (blk * P + k)
            valid = (dist >= 0) & (dist <= P)
            bucket = _t5_bucket(np.clip(dist, 0, P) * d)
            g = rel_bias[bucket]
            biasT[:, di, :, blk * P:(blk + 1) * P] = np.transpose(g, (0, 2, 1))
            mask[:, di, blk * P:(blk + 1) * P] = valid.astype(np.float32)
    return biasT, mask


def _host_consts(inp):
    cols = np.zeros((P, 160), np.float32)
    cols[:, 0:8] = np.asarray(inp["norm1_g"])[0].reshape(8, P).T
    cols[:, 8:16] = np.asarray(inp["norm2_g"])[0].reshape(8, P).T
    cw = np.asarray(inp["conv_w"])[0]
    for cc in range(4):
        cols[:, 16 + CW * cc:16 + CW * (cc + 1)] = cw[:, P * cc:P * (cc + 1)].T
    cols[:, 140:144] = np.asarray(inp["conv_b"])[0].reshape(4, P).T
    cols[:, 144:148] = np.asarray(inp["conv_ln_g"])[0].reshape(4, P).T
    cols[:, 148:152] = np.asarray(inp["conv_ln_b"])[0].reshape(4, P).T
    cols[:, 152] = EPS_RMS
    cols[:, 153] = EPS_LN
    gfin = np.ascontiguousarray(np.broadcast_to(np.asarray(inp["final_g"])[None, :], (P, D))).astype(np.float32)
    biasT, mask = _bias_tables(np.asarray(inp["rel_bias"], np.float32))
    return dict(
        w_in=np.ascontiguousarray(np.asarray(inp["w_in"], np.float32)[0]),
        w_o=np.ascontiguousarray(np.asarray(inp["w_o"], np.float32)[0]),
        w_ff1=np.ascontiguousarray(np.asarray(inp["w_ff1"], np.float32)[0]),
        w_ff2=np.ascontiguousarray(np.asarray(inp["w_ff2"], np.float32)[0]),
        cols=cols, gfin=gfin, biasT=biasT, mask=mask, ident=np.eye(P, dtype=np.float32),
    )


_CACHE = {}


def _get_nc(nseq, dump=()):
    key = (nseq, tuple(dump))
    if key not in _CACHE:
        b = Builder(nseq, dump)
        b.build()
        _CACHE[key] = b
    return _CACHE[key]


def kernel(**inputs):
    x = np.ascontiguousarray(np.asarray(inputs["x"], np.float32))
    B = x.shape[0]
    nseq = B // NCORES
    b = _get_nc(nseq)
    consts = _host_consts(inputs)
    in_maps = []
    for c in range(NCORES):
        m = dict(consts)
        m["x"] = np.ascontiguousarray(x[c * nseq:(c + 1) * nseq])
        in_maps.append(m)
    res = run_bass_kernel_spmd(b.nc, in_maps, core_ids=list(range(NCORES)))
    out = np.concatenate([np.asarray(r["out"]) for r in res.results], axis=0)
    return out.astype(np.float32)
```
